# Optimizing a Trainium2 kernel written in Bass

```python
import math
import jax
import jax.numpy as jnp
from jax import lax
import numpy as np

D_MODEL = 2048
BATCH = 4
SEQ = 8192
DEPTH = 4

N_EVEN = (DEPTH + 1) // 2
N_ODD = DEPTH // 2
BRANCH = D_MODEL // 2
RMS_EPS = 1e-6
NEG = -1e30
GATE_FLOOR = 1e-20
Q_BLOCK = 128

HG_HEADS = 8
HG_DK = BRANCH // HG_HEADS
HG_DV = BRANCH // HG_HEADS
HG_CHUNK = 64

DF_HEADS = 8
DF_DH = BRANCH // (2 * DF_HEADS)
DF_DV = 2 * DF_DH

GD_HEADS = 8
GD_DK = BRANCH // GD_HEADS
GD_DV = BRANCH // GD_HEADS
GD_CONV = 4
GD_CHUNK = 64

SB_HEADS = 8
SB_DH = BRANCH // SB_HEADS

EVEN_IN = 8 * BRANCH
ODD_SIZES = (BRANCH, BRANCH, BRANCH, BRANCH, GD_HEADS, GD_HEADS, BRANCH, BRANCH, BRANCH, BRANCH)
ODD_IN = sum(ODD_SIZES)

kernel_name = 'hybrid_hgrn2_diffattn_gdn_stickbreak'


def rmsnorm(x, g):
    xf = x.astype(jnp.float32)
    y = xf * lax.rsqrt(jnp.mean(xf * xf, axis=-1, keepdims=True) + RMS_EPS)
    return (y * g.astype(jnp.float32)).astype(x.dtype)


def l2norm(x):
    return x * lax.rsqrt(jnp.sum(x * x, axis=-1, keepdims=True) + RMS_EPS)


def to_heads(t, n_heads):
    b, l, _ = t.shape
    return t.reshape(b, l, n_heads, -1).transpose(0, 2, 1, 3)


def from_heads(t):
    b, h, l, d = t.shape
    return t.transpose(0, 2, 1, 3).reshape(b, l, h * d)


def to_chunks(t, c):
    b, h, l, d = t.shape
    return jnp.moveaxis(t.reshape(b, h, l // c, c, d), 2, 0)


def from_chunks(t):
    n, b, h, c, d = t.shape
    return jnp.moveaxis(t, 0, 2).reshape(b, h, n * c, d)


def causal_conv(x, w):
    return lax.conv_general_dilated(
        x, w[:, None, :], window_strides=(1,), padding=[(GD_CONV - 1, 0)],
        dimension_numbers=('NWC', 'WIO', 'NWC'), feature_group_count=x.shape[-1])


def hgrn2_recurrence(q, k, log_f, v):
    b, h, l, dk = q.shape
    dv = v.shape[-1]
    causal = jnp.tril(jnp.ones((HG_CHUNK, HG_CHUNK), dtype=bool))[:, :, None]

    def step(state, xs):
        qc, kc, gc, vc = xs
        cum = jnp.cumsum(gc, axis=-2)
        rel = cum[:, :, :, None, :] - cum[:, :, None, :, :]
        decay = jnp.exp(jnp.where(causal, rel, NEG))
        scores = jnp.einsum('bhtd,bhsd,bhtsd->bhts', qc, kc, decay)
        out = (jnp.einsum('bhts,bhsv->bhtv', scores, vc)
               + jnp.einsum('bhtd,bhdv->bhtv', qc * jnp.exp(cum), state))
        last = cum[:, :, -1:, :]
        state = (jnp.exp(last[:, :, 0, :])[..., None] * state
                 + jnp.einsum('bhsd,bhsv->bhdv', kc * jnp.exp(last - cum), vc))
        return state, out

    s0 = jnp.zeros((b, h, dk, dv), jnp.float32)
    xs = (to_chunks(q, HG_CHUNK), to_chunks(k, HG_CHUNK), to_chunks(log_f, HG_CHUNK), to_chunks(v, HG_CHUNK))
    _, out = lax.scan(step, s0, xs)
    return from_chunks(out)


def diff_attention(q_raw, k_raw, v_raw, lam_vec, lam_init):
    b, l, _ = q_raw.shape
    nb = l // Q_BLOCK
    q = q_raw.reshape(b, l, DF_HEADS, 2, DF_DH).transpose(0, 2, 3, 1, 4)
    k = k_raw.reshape(b, l, DF_HEADS, 2, DF_DH).transpose(0, 2, 3, 1, 4)
    v = to_heads(v_raw, DF_HEADS)
    lam = (jnp.exp(jnp.sum(lam_vec[0] * lam_vec[1])) - jnp.exp(jnp.sum(lam_vec[2] * lam_vec[3]))
           + lam_init)
    qb = jnp.moveaxis(q.reshape(b, DF_HEADS, 2, nb, Q_BLOCK, DF_DH), 3, 0)
    key_pos = jnp.arange(l)
    scale = DF_DH ** -0.5

    def block(args):
        q_blk, bi = args
        s = jnp.einsum('bhcqd,bhckd->bhcqk', q_blk, k) * scale
        q_pos = bi * Q_BLOCK + jnp.arange(Q_BLOCK)
        mask = key_pos[None, :] <= q_pos[:, None]
        p = jax.nn.softmax(jnp.where(mask, s, NEG), axis=-1)
        wts = p[:, :, 0] - lam * p[:, :, 1]
        return jnp.einsum('bhqk,bhkv->bhqv', wts, v)

    out = lax.map(block, (qb, jnp.arange(nb)))
    return from_chunks(out)


def gated_delta_rule(q, k, v, beta, g):
    b, h, l, dk = q.shape
    dv = v.shape[-1]
    c = GD_CHUNK
    n = l // c
    qc = q.reshape(b, h, n, c, dk)
    kc = k.reshape(b, h, n, c, dk)
    vc = v.reshape(b, h, n, c, dv)
    bc = beta.reshape(b, h, n, c)[..., None]
    gc = jnp.cumsum(g.reshape(b, h, n, c), axis=-1)
    incl = jnp.tril(jnp.ones((c, c), dtype=bool))
    strict = jnp.tril(jnp.ones((c, c), dtype=bool), -1)
    decay = jnp.exp(jnp.where(incl, gc[..., :, None] - gc[..., None, :], NEG))
    kb = kc * bc
    lower = jnp.where(strict, jnp.einsum('bhnid,bhnjd->bhnij', kb, kc) * decay, 0.0)
    eye = jnp.eye(c, dtype=q.dtype)
    tinv = lax.linalg.triangular_solve(eye + lower, jnp.broadcast_to(eye, lower.shape),
                                       left_side=True, lower=True)
    u = tinv @ (vc * bc)
    w = tinv @ (kb * jnp.exp(gc)[..., None])
    attn = jnp.einsum('bhnid,bhnjd->bhnij', qc, kc) * decay
    qg = qc * jnp.exp(gc)[..., None]
    kdec = kc * jnp.exp(gc[..., -1:] - gc)[..., None]
    glast = jnp.exp(gc[..., -1])[..., None, None]

    def step(state, xs):
        u_n, w_n, a_n, qg_n, kd_n, gl_n = xs
        v_new = u_n - w_n @ state
        out = qg_n @ state + a_n @ v_new
        state = gl_n * state + jnp.swapaxes(kd_n, -1, -2) @ v_new
        return state, out

    xs = tuple(jnp.moveaxis(t, 2, 0) for t in (u, w, attn, qg, kdec, glast))
    s0 = jnp.zeros((b, h, dk, dv), jnp.float32)
    _, out = lax.scan(step, s0, xs)
    return from_chunks(out)


def stick_breaking(q_raw, k_raw, v_raw):
    b, l, _ = q_raw.shape
    nb = l // Q_BLOCK
    q = to_heads(q_raw, SB_HEADS)
    k = to_heads(k_raw, SB_HEADS)
    v = to_heads(v_raw, SB_HEADS)
    qb = to_chunks(q, Q_BLOCK)
    key_pos = jnp.arange(l)
    scale = SB_DH ** -0.5

    def block(args):
        q_blk, bi = args
        z = jnp.einsum('bhqd,bhkd->bhqk', q_blk, k) * scale
        q_pos = bi * Q_BLOCK + jnp.arange(Q_BLOCK)
        earlier = key_pos[None, :] < q_pos[:, None]
        log_rest = jnp.where(earlier, jax.nn.log_sigmoid(-z), 0.0)
        after = lax.cumsum(log_rest, axis=3, reverse=True) - log_rest
        wts = jnp.where(earlier, jnp.exp(jax.nn.log_sigmoid(z) + after), 0.0)
        return jnp.einsum('bhqk,bhkv->bhqv', wts, v)

    return from_chunks(lax.map(block, (qb, jnp.arange(nb))))


def even_layer(h, w_in, w_out, lb, hg_gain, lam_vec, df_gain, layer_idx):
    f32 = jnp.float32
    proj = jnp.einsum('bld,de->ble', h, w_in).astype(f32)
    hq, hf, hi, hz, dq, dk, dv, dz = jnp.split(proj, 8, axis=-1)
    lb_h = lb.astype(f32).reshape(HG_HEADS, 1, HG_DK)
    fp = to_heads(hf, HG_HEADS)
    sig = jax.nn.sigmoid(fp)
    log_f = jnp.log(jnp.maximum(lb_h + (1.0 - lb_h) * sig, GATE_FLOOR))
    k_in = (1.0 - lb_h) * (1.0 - sig)
    o_a = hgrn2_recurrence(to_heads(jax.nn.silu(hq), HG_HEADS), k_in, log_f, to_heads(hi, HG_HEADS))
    o_a = from_heads(rmsnorm(o_a, hg_gain)) * jax.nn.silu(hz)
    lam_init = 0.8 - 0.6 * math.exp(-0.3 * layer_idx)
    o_b = diff_attention(dq, dk, dv, lam_vec.astype(f32), lam_init)
    o_b = from_heads(rmsnorm(o_b, df_gain)) * (1.0 - lam_init) * jax.nn.silu(dz)
    mixed = jnp.concatenate([o_a, o_b], axis=-1).astype(h.dtype)
    return jnp.einsum('ble,ed->bld', mixed, w_out)


def odd_layer(h, w_in, w_out, conv_w, a_log, dt_bias, gd_gain):
    f32 = jnp.float32
    proj = jnp.einsum('bld,de->ble', h, w_in).astype(f32)
    splits = [int(s) for s in np.cumsum(ODD_SIZES)[:-1]]
    cq, ck, cv, cz, ca, cb, sq, sk, sv, sz = jnp.split(proj, splits, axis=-1)
    qkv = jax.nn.silu(causal_conv(jnp.concatenate([cq, ck, cv], axis=-1), conv_w.astype(f32)))
    cq, ck, cv = jnp.split(qkv, 3, axis=-1)
    q = l2norm(to_heads(cq, GD_HEADS)) * (GD_DK ** -0.5)
    k = l2norm(to_heads(ck, GD_HEADS))
    v = to_heads(cv, GD_HEADS)
    beta = jax.nn.sigmoid(cb).transpose(0, 2, 1)
    g = (-jnp.exp(a_log.astype(f32)) * jax.nn.softplus(ca + dt_bias.astype(f32))).transpose(0, 2, 1)
    o_c = gated_delta_rule(q, k, v, beta, g)
    o_c = from_heads(rmsnorm(o_c, gd_gain)) * jax.nn.silu(cz)
    o_d = from_heads(stick_breaking(sq, sk, sv)) * jax.nn.silu(sz)
    mixed = jnp.concatenate([o_c, o_d], axis=-1).astype(h.dtype)
    return jnp.einsum('ble,ed->bld', mixed, w_out)


def setup_inputs(seed: int = 0) -> dict:
    key = jax.random.key(seed)
    ks = jax.random.split(key, 16)
    f32 = jnp.float32

    def gain(k, shape):
        return 1.0 + 0.02 * jax.random.normal(k, shape, f32)

    x = jax.random.normal(ks[0], (BATCH, SEQ, D_MODEL), f32)
    norm_pre = gain(ks[1], (DEPTH, D_MODEL))
    norm_post = gain(ks[2], (DEPTH, D_MODEL))
    ev_w_in = jax.random.normal(ks[3], (N_EVEN, D_MODEL, EVEN_IN), f32) * D_MODEL ** -0.5
    ev_w_out = jax.random.normal(ks[4], (N_EVEN, 2 * BRANCH, D_MODEL), f32) * (2 * BRANCH) ** -0.5
    hg_lb_logits = 0.5 * jax.random.normal(ks[5], (N_EVEN, BRANCH), f32)
    hg_norm = gain(ks[6], (N_EVEN, HG_DV))
    df_lambda = 0.1 * jax.random.normal(ks[7], (N_EVEN, 4, DF_DH), f32)
    df_norm = gain(ks[8], (N_EVEN, DF_DV))
    od_w_in = jax.random.normal(ks[9], (N_ODD, D_MODEL, ODD_IN), f32) * D_MODEL ** -0.5
    od_w_out = jax.random.normal(ks[10], (N_ODD, 2 * BRANCH, D_MODEL), f32) * (2 * BRANCH) ** -0.5
    gd_conv = jax.random.normal(ks[11], (N_ODD, GD_CONV, 3 * BRANCH), f32) * GD_CONV ** -0.5
    gd_a_log = jnp.log(jax.random.uniform(ks[12], (N_ODD, GD_HEADS), f32, 1.0, 16.0))
    dt = jnp.exp(jax.random.uniform(ks[13], (N_ODD, GD_HEADS), f32, math.log(1e-3), math.log(1e-1)))
    gd_dt_bias = dt + jnp.log(-jnp.expm1(-dt))
    gd_norm = gain(ks[14], (N_ODD, GD_DV))
    return {'x': x, 'norm_pre': norm_pre, 'norm_post': norm_post,
            'ev_w_in': ev_w_in, 'ev_w_out': ev_w_out, 'hg_lb_logits': hg_lb_logits,
            'hg_norm': hg_norm, 'df_lambda': df_lambda, 'df_norm': df_norm,
            'od_w_in': od_w_in, 'od_w_out': od_w_out, 'gd_conv': gd_conv,
            'gd_a_log': gd_a_log, 'gd_dt_bias': gd_dt_bias, 'gd_norm': gd_norm}


def reference(x, norm_pre, norm_post, ev_w_in, ev_w_out, hg_lb_logits, hg_norm, df_lambda, df_norm,
              od_w_in, od_w_out, gd_conv, gd_a_log, gd_dt_bias, gd_norm):
    p = jax.nn.softmax(hg_lb_logits.astype(jnp.float32), axis=0)
    lower_bounds = jnp.cumsum(p, axis=0) - p[0:1]
    for layer in range(DEPTH):
        h = rmsnorm(x, norm_pre[layer])
        j = layer // 2
        if layer % 2 == 0:
            y = even_layer(h, ev_w_in[j], ev_w_out[j], lower_bounds[j], hg_norm[j],
                           df_lambda[j], df_norm[j], layer)
        else:
            y = odd_layer(h, od_w_in[j], od_w_out[j], gd_conv[j], gd_a_log[j],
                          gd_dt_bias[j], gd_norm[j])
        x = x + rmsnorm(y, norm_post[layer])
    return x
```

```python
import math
import numpy as np
import concourse.bass as bass
import concourse.mybir as mybir
from concourse.bass_utils import run_bass_kernel_spmd


F32 = mybir.dt.float32
BF16 = mybir.dt.bfloat16
AF = mybir.ActivationFunctionType
ALU = mybir.AluOpType
AX = mybir.AxisListType

ENGS = ("pe", "act", "dve", "pool", "sp")


class Buf:
    __slots__ = ("name", "w", "r", "const")

    def __init__(self, name, const=False):
        self.name = name
        self.w = None
        self.r = {}
        self.const = const


class Sched:
    def __init__(self, nc, ndma_sems=24):
        self.nc = nc
        self.ops = []
        self.cuts = []
        self.seg_ops = 2000
        self.ndma_sems = ndma_sems

    def op(self, eng, fn, reads=(), writes=(), dma=False):
        i = len(self.ops)
        deps = set()
        for b in reads:
            if b.w is not None:
                deps.add(b.w)
        for b in writes:
            if b.w is not None:
                deps.add(b.w)
            deps.update(b.r.values())
        for b in reads:
            if not b.const:
                key = (eng, i) if dma else eng
                b.r[key] = i
        for b in writes:
            b.w = i
            b.r = {}
        self.ops.append([eng, fn, dma, deps])
        return i

    def emit(self, final_waits=()):
        nc = self.nc
        ops = self.ops
        n = len(ops)
        need_signal = [False] * n
        for i, (eng, fn, dma, deps) in enumerate(ops):
            nd = set()
            for d in deps:
                deng, _, ddma, _ = ops[d]
                if (not ddma) and (not dma) and deng == eng and eng == 'pe':
                    continue
                nd.add(d)
                if not ddma:
                    need_signal[d] = True
            ops[i][3] = nd
        for d in final_waits:
            if not ops[d][2]:
                need_signal[d] = True
        sigval = [None] * n
        cnt = {e: 0 for e in ENGS}
        dcnt = {e: 0 for e in ENGS}
        dma_idx = [None] * n
        for i, (eng, fn, dma, deps) in enumerate(ops):
            if dma:
                dma_idx[i] = dcnt[eng]
                dcnt[eng] += 1
            elif need_signal[i]:
                cnt[eng] += 1
                sigval[i] = cnt[eng]
        K = self.ndma_sems
        import contextlib
        with contextlib.ExitStack() as es:
            csem = {e: es.enter_context(nc.semaphore("c_" + e)) for e in ENGS if cnt[e] > 0}
            dsem = {}
            for e in ENGS:
                if dcnt[e] > 0:
                    dsem[e] = [es.enter_context(nc.semaphore("d_%s_%d" % (e, k))) for k in range(min(K, dcnt[e]))]
            def target(d):
                deng, _, ddma, _ = ops[d]
                if ddma:
                    j = dma_idx[d]
                    return dsem[deng][j % K], 16 * (j // K + 1)
                return csem[deng], sigval[d]

            bounds = list(range(0, n, self.seg_ops)) + [n]
            seen_all = {e: {} for e in ENGS}

            def make(eng_name, lo, hi, is_last):
                def body(e):
                    seen = seen_all[eng_name]
                    for i in range(lo, hi):
                        eng, fn, dma, deps = ops[i]
                        if eng != eng_name:
                            continue
                        waits = {}
                        for d in deps:
                            s, v = target(d)
                            key = id(s)
                            if seen.get(key, 0) >= v:
                                continue
                            if key not in waits or waits[key][1] < v:
                                waits[key] = (s, v)
                        if dma:
                            j = dma_idx[i]
                            if j >= K:
                                s = dsem[eng][j % K]
                                v = 16 * (j // K)
                                key = id(s)
                                if seen.get(key, 0) < v and (key not in waits or waits[key][1] < v):
                                    waits[key] = (s, v)
                        wl = list(waits.items())
                        inline = None
                        if wl and eng_name != "pe":
                            inline = wl.pop()
                        for key, (s, v) in wl:
                            e.wait_ge(s, v)
                            seen[key] = v
                        ins = fn(e)
                        if inline is not None:
                            ins._wait_ge(inline[1][0], inline[1][1])
                            seen[inline[0]] = inline[1][1]
                        if dma:
                            j = dma_idx[i]
                            ins.then_inc(dsem[eng][j % K], 16)
                        elif sigval[i] is not None:
                            ins.then_inc(csem[eng], 1)
                    if eng_name == "sp" and is_last:
                        for d in final_waits:
                            s, v = target(d)
                            e.wait_ge(s, v)
                return body

            for si in range(len(bounds) - 1):
                lo, hi = bounds[si], bounds[si + 1]
                is_last = (si == len(bounds) - 2)
                with nc.Block() as block:
                    block.tensor(make("pe", lo, hi, is_last))
                    block.scalar(make("act", lo, hi, is_last))
                    block.vector(make("dve", lo, hi, is_last))
                    block.gpsimd(make("pool", lo, hi, is_last))
                    block.sync(make("sp", lo, hi, is_last))
        return cnt, dcnt


D = 2048
KC = D // 128
EPS = 1e-6


class Ctx:
    pass


class K:
    def __init__(self, nc):
        self.nc = nc
        self.S = Sched(nc)
        self.phase = Buf("phase")
        self.nb = 0

    def buf(self, name=None):
        self.nb += 1
        return Buf(name or "b%d" % self.nb)

    def _op(self, eng, fn, R, W, dma=False):
        return self.S.op(eng, fn, reads=[b for b in R if b is not None] + [self.phase], writes=[b for b in W if b is not None], dma=dma)

    def barrier(self, scratch):
        i = self.S.op("pool", lambda e: e.memset(scratch, 0.0), reads=[], writes=[self.phase])
        self.S.cuts.append(i)
        return i

    def dma(self, eng, out, in_, R, W, slow=False):
        if slow:
            return self._op(eng, lambda e: e.dma_start(out=out, in_=in_, allow_slow_non_contiguous=True), R, W, dma=True)
        return self._op(eng, lambda e: e.dma_start(out=out, in_=in_), R, W, dma=True)

    def dma3(self, eng, out3, in3, R, W, step=16):
        n = out3.shape[1]
        for a in range(0, n, step):
            b = min(a + step, n)
            self.dma(eng, out3[:, a:b, :], in3[:, a:b, :], R, W)

    def mm(self, out, lhsT, rhs, start, stop, R, W, skip=False):
        if skip:
            return self._op("pe", lambda e: e.matmul(out, lhsT, rhs, start=start, stop=stop, skip_group_check=True), R, W)
        return self._op("pe", lambda e: e.matmul(out, lhsT, rhs, start=start, stop=stop), R, W)

    def tr(self, out, in_, ident, R, W):
        return self._op("pe", lambda e: e.transpose(out, in_, ident), R, W)

    def act(self, out, in_, func, R, W, bias=None, scale=None, accum=None):
        kw = {}
        if bias is not None:
            kw["bias"] = bias
        if scale is not None:
            kw["scale"] = scale
        if accum is not None:
            kw["accum_out"] = accum
        return self._op("act", lambda e: e.activation(out=out, in_=in_, func=func, **kw), R, W)

    def tt(self, eng, out, in0, in1, op, R, W):
        return self._op(eng, lambda e: e.tensor_tensor(out=out, in0=in0, in1=in1, op=op), R, W)

    def ts(self, eng, out, in0, s1, s2, op0, op1, R, W):
        if s2 is None:
            return self._op(eng, lambda e: e.tensor_scalar(out=out, in0=in0, scalar1=s1, scalar2=None, op0=op0), R, W)
        return self._op(eng, lambda e: e.tensor_scalar(out=out, in0=in0, scalar1=s1, scalar2=s2, op0=op0, op1=op1), R, W)

    def stt(self, eng, out, in0, scalar, in1, op0, op1, R, W):
        return self._op(eng, lambda e: e.scalar_tensor_tensor(out=out, in0=in0, scalar=scalar, in1=in1, op0=op0, op1=op1), R, W)

    def cp(self, eng, out, in_, R, W):
        if eng == "act":
            return self._op("act", lambda e: e.activation(out=out, in_=in_, func=AF.Copy), R, W)
        return self._op(eng, lambda e: e.tensor_copy(out=out, in_=in_), R, W)

    def memset(self, eng, out, val, W):
        return self._op(eng, lambda e: e.memset(out, val), [], W)

    def recip(self, out, in_, R, W):
        return self._op("dve", lambda e: e.reciprocal(out=out, in_=in_), R, W)

    def red(self, out, in_, op, R, W):
        return self._op("dve", lambda e: e.tensor_reduce(out=out, in_=in_, axis=AX.X, op=op), R, W)

    def scan(self, out, d0, d1, init, op0, op1, R, W):
        return self._op("dve", lambda e: e.tensor_tensor_scan(out=out, data0=d0, data1=d1, initial=init, op0=op0, op1=op1), R, W)

    def asel(self, out, in_, pattern, cmp, fill, base, cm, R, W):
        return self._op("pool", lambda e: e.affine_select(out=out, in_=in_, pattern=pattern, compare_op=cmp, fill=fill, base=base, channel_multiplier=cm), R, W)


class Arena:
    def __init__(self, nc, nbytes, base=32 * 1024):
        self.nc = nc
        self.nbytes = nbytes
        self.base = base
        self.n = 0
        self.reset()

    def reset(self):
        self.off = 0

    def alloc(self, n, dt):
        sz = 4 if dt == F32 else 2
        b = (n * sz + 31) // 32 * 32
        assert self.off + b <= self.nbytes, ("arena overflow", self.off, b)
        self.n += 1
        t = self.nc.alloc_sbuf_tensor_at("ar%d" % self.n, [128, b // sz], dt, offset=self.base + self.off)
        self.off += b
        return t[:, 0:n]


def rstd_from_ssq(k, ssq, tmp, out, n, R, W):
    k.ts("dve", tmp, ssq, 1.0 / n, EPS, ALU.mult, ALU.add, R, W)
    k.recip(tmp, tmp, W, W)
    k.act(out, tmp, AF.Sqrt, W, W)


def setup_consts(k, c):
    nc = k.nc
    c.ident_f = nc.alloc_sbuf_tensor("ident_f", [128, 128], F32)
    c.ident = nc.alloc_sbuf_tensor("ident", [128, 128], BF16)
    c.tri_f = nc.alloc_sbuf_tensor("tri_f", [128, 128], F32)
    c.tri = nc.alloc_sbuf_tensor("tri", [128, 128], BF16)
    c.bd = nc.alloc_sbuf_tensor("bd", [128, 128], BF16)
    c.zeros = nc.alloc_sbuf_tensor("zeros", [128, 512], BF16)
    c.scr = nc.alloc_sbuf_tensor("scr", [128, 8], F32)
    c.B = k.buf("consts")
    B = [c.B]
    k.memset("pool", c.ident_f[:], 0.0, B)
    k.asel(c.ident_f[:], c.ident_f[:], [[-1, 128]], ALU.not_equal, 1.0, 0, 1, B, B)
    k.cp("pool", c.ident[:], c.ident_f[:], B, B)
    k.memset("pool", c.tri_f[:], 1.0, B)
    k.asel(c.tri_f[:], c.tri_f[:], [[1, 128]], ALU.is_ge, 0.0, 0, -1, B, B)
    k.cp("pool", c.tri[:], c.tri_f[:], B, B)
    k.cp("pool", c.bd[:], c.tri_f[:], B, B)
    k.memset("pool", c.bd[0:64, 64:128], 0.0, B)
    k.memset("pool", c.zeros[:], 0.0, B)
    c.B.const = True


def phase_prenorm(k, c, x_src, gain_d, li, L):
    ar = c.arena
    ar.reset()
    k.barrier(c.scr[:, 0:1])
    xt = [ar.alloc(D, F32) for _ in range(2)]
    xb = [ar.alloc(D, BF16) for _ in range(2)]
    junk = ar.alloc(D, BF16)
    hTt = [ar.alloc(KC * 128, BF16) for _ in range(2)]
    gT = ar.alloc(KC, F32)
    st = ar.alloc(8, F32)
    Bxt = [k.buf() for _ in range(2)]
    Bxb = [k.buf() for _ in range(2)]
    Bh = [k.buf() for _ in range(2)]
    Bj, Bg, Bst = k.buf(), k.buf(), k.buf()
    Bps = c.Bpst
    k.dma("sp", gT, gain_d[li, :].rearrange("(kc p) -> p kc", p=128), [], [Bg], slow=True)
    for t in range(L // 128):
        s = t % 2
        k.dma("sp", xt[s], x_src[t * 128:(t + 1) * 128, :], [], [Bxt[s]])
        k.act(junk, xt[s], AF.Square, [Bxt[s]], [Bj, Bst], accum=st[:, 0:1])
        rstd_from_ssq(k, st[:, 0:1], st[:, 1:2], st[:, 2:3], D, [Bst], [Bst])
        k.act(xb[s], xt[s], AF.Copy, [Bxt[s], Bst], [Bxb[s]], scale=st[:, 2:3])
        for half in range(2):
            ps = c.pst[half]
            for j in range(8):
                kc = half * 8 + j
                k.tr(ps[:, j * 128:(j + 1) * 128], xb[s][:, kc * 128:(kc + 1) * 128], c.ident[:], [Bxb[s], c.B], [Bps[half]])
            k.tt("dve", hTt[s][:, half * 1024:(half + 1) * 1024].rearrange("p (a b) -> p a b", b=128),
                 ps[:, :].rearrange("p (a b) -> p a b", b=128),
                 gT[:, half * 8:(half + 1) * 8].unsqueeze(2).to_broadcast([128, 8, 128]), ALU.mult,
                 [Bps[half], Bg], [Bh[s]])
        k.dma("pool", c.hT[:, :, t * 128:(t + 1) * 128].rearrange("kc p t -> p kc t"),
              hTt[s].rearrange("p (kc t) -> p kc t", t=128), [Bh[s]], [c.BhT])


def phase_inproj(k, c, w_d, slots, L):
    ar = c.arena
    HW = c.HW
    NT = L // 512
    for si, slot in enumerate(slots):
        ar.reset()
        k.barrier(c.scr[:, 0:1])
        wb = ar.alloc(KC * HW, BF16)
        wst = [ar.alloc(2 * HW, F32) for _ in range(2)]
        hb = [ar.alloc(KC * 512, BF16) for _ in range(2)]
        Bw, Bwst, Bhb = k.buf(), [k.buf(), k.buf()], [k.buf(), k.buf()]
        wbv = wb.rearrange("p (kc n) -> p kc n", n=HW)
        col0 = si * HW
        for q in range(KC // 2):
            s = q % 2
            k.dma("sp", wst[s].rearrange("p (a n) -> p a n", n=HW),
                  w_d[q * 256:(q + 1) * 256, col0:col0 + HW].rearrange("(a p) n -> p a n", p=128), [], [Bwst[s]])
            k.cp("pool", wbv[:, q * 2:(q + 1) * 2, :], wst[s].rearrange("p (a n) -> p a n", n=HW), [Bwst[s]], [Bw])
        epi = slot["epi"](k, c, ar)
        for tb in range(NT):
            s = tb % 2
            k.dma("sp", hb[s].rearrange("p (kc t) -> p kc t", t=512),
                  c.hT[:, :, tb * 512:(tb + 1) * 512].rearrange("kc p t -> p kc t"), [c.BhT], [Bhb[s]])
            hv = hb[s].rearrange("p (kc t) -> p kc t", t=512)
            if slot["kind"] == "fm":
                for cg in range(HW // 128):
                    pb = c.nextbank()
                    for kc in range(KC):
                        k.mm(c.ps[pb][:, :], wbv[:, kc, cg * 128:(cg + 1) * 128], hv[:, kc, :], kc == 0, kc == KC - 1,
                             [Bw, Bhb[s]], [c.Bps[pb]])
                    epi(cg, tb, c.ps[pb], c.Bps[pb])
            else:
                for tt_ in range(4):
                    for cg in range(HW // 512):
                        pb = c.nextbank()
                        for kc in range(KC):
                            k.mm(c.ps[pb][:, :], hv[:, kc, tt_ * 128:(tt_ + 1) * 128], wbv[:, kc, cg * 512:(cg + 1) * 512],
                                 kc == 0, kc == KC - 1, [Bw, Bhb[s]], [c.Bps[pb]])
                        epi(cg, tb * 4 + tt_, c.ps[pb], c.Bps[pb])


def epi_fm_simple(dst, Bdst, func, scale=None):
    def make(k, c, ar):
        ob = [ar.alloc(512, BF16) for _ in range(2)]
        Bo = [k.buf(), k.buf()]
        cnt = [0]

        def epi(cg, tb, ps, Bp):
            s = cnt[0] % 2
            cnt[0] += 1
            k.act(ob[s], ps[:, :], func, [Bp], [Bo[s]], scale=scale)
            k.dma("pool", dst[cg, :, tb * 512:(tb + 1) * 512], ob[s], [Bo[s]], [Bdst])
        return epi
    return make


def epi_tm_simple(dst, Bdst, func):
    def make(k, c, ar):
        ob = [ar.alloc(512, BF16) for _ in range(2)]
        Bo = [k.buf(), k.buf()]
        cnt = [0]

        def epi(cg, tt_, ps, Bp):
            s = cnt[0] % 2
            cnt[0] += 1
            k.act(ob[s], ps[:, :], func, [Bp], [Bo[s]])
            k.dma("pool", dst[tt_ * 128:(tt_ + 1) * 128, cg * 512:(cg + 1) * 512], ob[s], [Bo[s]], [Bdst])
        return epi
    return make


def epi_hgrn_f(lb_d, j, gT_dst, kT_dst, Bdst):
    def make(k, c, ar):
        NHL = c.NHL
        lg = ar.alloc(2 * NHL, F32)
        ex = ar.alloc(2 * NHL, F32)
        lbt = ar.alloc(NHL, F32)
        oml = ar.alloc(NHL, F32)
        tmp = ar.alloc(NHL, F32)
        Bl = k.buf()
        lgv = lg.rearrange("p (l h) -> p l h", h=NHL)
        exv = ex.rearrange("p (l h) -> p l h", h=NHL)
        k.dma("sp", lgv, lb_d.rearrange("l (h p) -> p l h", p=128), [], [Bl], slow=True)
        k.act(ex, lg, AF.Exp, [Bl], [Bl])
        k.tt("dve", tmp, exv[:, 0, :], exv[:, 1, :], ALU.add, [Bl], [Bl])
        k.recip(tmp, tmp, [Bl], [Bl])
        if j == 0:
            k.memset("dve", lbt, 0.0, [Bl])
        else:
            k.tt("dve", lbt, exv[:, 1, :], tmp, ALU.mult, [Bl], [Bl])
        k.ts("dve", oml, lbt, -1.0, 1.0, ALU.mult, ALU.add, [Bl], [Bl])
        sg = [ar.alloc(512, F32) for _ in range(2)]
        fo = [ar.alloc(512, F32) for _ in range(2)]
        ko = [ar.alloc(512, BF16) for _ in range(2)]
        Bs, Bf, Bk = [k.buf(), k.buf()], [k.buf(), k.buf()], [k.buf(), k.buf()]
        cnt = [0]

        def epi(cg, tb, ps, Bp):
            s = cnt[0] % 2
            cnt[0] += 1
            k.act(sg[s], ps[:, :], AF.Sigmoid, [Bp], [Bs[s]])
            k.ts("dve", fo[s], sg[s], oml[:, cg:cg + 1], lbt[:, cg:cg + 1], ALU.mult, ALU.add, [Bs[s], Bl], [Bf[s]])
            k.ts("dve", fo[s], fo[s], 1e-20, None, ALU.max, None, [Bf[s]], [Bf[s]])
            k.act(fo[s], fo[s], AF.Ln, [Bf[s]], [Bf[s]])
            k.dma("pool", gT_dst[cg, :, tb * 512:(tb + 1) * 512], fo[s], [Bf[s]], [Bdst])
            k.ts("dve", ko[s], sg[s], -1.0, 1.0, ALU.mult, ALU.add, [Bs[s]], [Bk[s]])
            k.ts("dve", ko[s], ko[s], oml[:, cg:cg + 1], None, ALU.mult, None, [Bk[s], Bl], [Bk[s]])
            k.dma("pool", kT_dst[cg, :, tb * 512:(tb + 1) * 512], ko[s], [Bk[s]], [Bdst])
        return epi
    return make


def make_ctx(k, NHL, L, nbank=6, arena_bytes=150 * 1024):
    nc = k.nc
    c = Ctx()
    c.NHL, c.HW, c.L = NHL, NHL * 128, L
    c.arena = Arena(nc, arena_bytes)
    c.ps = [nc.alloc_psum_tensor("ps%d" % i, [128, 512], F32) for i in range(nbank)]
    c.Bps = [k.buf("ps%d" % i) for i in range(nbank)]
    c.pst = [nc.alloc_psum_tensor("pst%d" % i, [128, 1024], BF16) for i in range(8 - nbank)]
    c.Bpst = [k.buf("pst%d" % i) for i in range(8 - nbank)]
    c._bank = [0]
    c.nbank_rot = nbank

    def nextbank():
        b = c._bank[0] % c.nbank_rot
        c._bank[0] += 1
        return b
    c.nextbank = nextbank
    c.hT = nc.dram_tensor("hT", [KC, 128, L], BF16).ap()
    c.BhT = k.buf("hT")
    setup_consts(k, c)
    return c


def alloc_scratch(k, c, L):
    nc = k.nc
    NHL, HW = c.NHL, c.HW
    for nm in ("qT", "kT", "dqT", "dkT", "sqT", "skT", "cqT", "ckT", "cvT", "vT"):
        setattr(c, nm, nc.dram_tensor(nm, [NHL, 128, L], BF16).ap())
    c.gT = nc.dram_tensor("gT", [NHL, 128, L], F32).ap()
    for nm in ("hv", "hz", "dv", "dz", "sv", "sz", "cz"):
        setattr(c, nm, nc.dram_tensor(nm, [L, HW], BF16).ap())
    c.mixed = nc.dram_tensor("mixed", [L, 2 * HW], BF16).ap()
    c.gb = nc.dram_tensor("gb", [L, 2 * NHL], F32).ap()


def head_epilogue(k, c, ps_o, Bp, st, Bst, gainb, Bg, zt_ap, Bz, o1, Bo1, om, Bom, dst_ap, junk, Bj, norm=True):
    if norm:
        k.act(junk, ps_o, AF.Square, [Bp], [Bj, Bst], accum=st[:, 0:1])
        rstd_from_ssq(k, st[:, 0:1], st[:, 1:2], st[:, 2:3], 128, [Bst], [Bst])
        k.stt("dve", o1, ps_o, st[:, 2:3], gainb, ALU.mult, ALU.mult, [Bp, Bst, Bg], [Bo1])
        k.tt("pool", om, o1, zt_ap, ALU.mult, [Bo1, Bz], [Bom])
    else:
        k.tt("dve", om, ps_o, zt_ap, ALU.mult, [Bp, Bz], [Bom])
    k.dma("pool", dst_ap, om, [Bom], [])


def phase_hgrn(k, c, gain_d, j, L, col0=0):
    ar = c.arena
    ar.reset()
    k.barrier(c.scr[:, 0:1])
    NHL = c.NHL
    NT = L // 512
    S32 = ar.alloc(128, F32)
    Sbf = [ar.alloc(128, BF16) for _ in range(2)]
    ones = ar.alloc(512, F32)
    gainb = ar.alloc(128, F32)
    st = ar.alloc(8, F32)
    khA, khB = ar.alloc(128, BF16), ar.alloc(128, BF16)
    qhA, qhB = ar.alloc(128, BF16), ar.alloc(128, BF16)
    scm = ar.alloc(128, BF16)
    junk = ar.alloc(128, BF16)
    o1 = ar.alloc(128, F32)
    om = [ar.alloc(128, BF16) for _ in range(2)]
    A, E = ar.alloc(512, F32), ar.alloc(512, F32)
    E4 = [ar.alloc(512, F32) for _ in range(2)]
    qt = [ar.alloc(512, BF16) for _ in range(2)]
    kt = [ar.alloc(512, BF16) for _ in range(2)]
    gt = [ar.alloc(512, F32) for _ in range(2)]
    GG = [ar.alloc(513, F32) for _ in range(2)]
    vt = [ar.alloc(512, BF16) for _ in range(2)]
    zt = [ar.alloc(512, BF16) for _ in range(2)]
    qtl, ktl, khT, qh = [ar.alloc(512, BF16) for _ in range(4)]
    B = lambda: k.buf()
    BS, BSb, Bc, Bst, Bkh, Bqh, Bscm, Bj, Bo1 = B(), [B(), B()], B(), B(), [B(), B()], [B(), B()], B(), B(), B()
    Bom, BA, BE, BE4, Bqt, Bkt, Bgt, BGG, Bvt, Bzt = [B(), B()], B(), B(), [B(), B()], [B(), B()], [B(), B()], [B(), B()], [B(), B()], [B(), B()], [B(), B()]
    Bqtl, Bktl, BkhT, Bqhh = B(), B(), B(), B()
    k.memset("pool", ones, 1.0, [Bc])
    k.dma("sp", gainb, gain_d[j:j + 1, :].to_broadcast([128, 128]), [], [Bc], slow=True)
    k.memset("pool", khA, 0.0, [Bkh[0]])
    k.memset("pool", khB, 0.0, [Bkh[1]])
    k.memset("pool", qhA, 0.0, [Bqh[0]])
    k.memset("pool", qhB, 0.0, [Bqh[1]])
    v3 = lambda ap: ap.rearrange("p (c j) -> p c j", j=64)
    it = 0
    for h in range(NHL):
        k.memset("dve", S32, 0.0, [BS])
        k.memset("pool", Sbf[0], 0.0, [BSb[0]])
        for tb in range(NT):
            s = it % 2
            it += 1
            tsl = slice(tb * 512, (tb + 1) * 512)
            k.dma("sp", qt[s], c.qT[h, :, tsl], [], [Bqt[s]])
            k.dma("sp", kt[s], c.kT[h, :, tsl], [], [Bkt[s]])
            k.dma("sp", gt[s], c.gT[h, :, tsl], [], [Bgt[s]])
            k.dma("sp", vt[s].rearrange("p (a d) -> p a d", d=128),
                  c.hv[tsl, h * 128:(h + 1) * 128].rearrange("(a p) d -> p a d", p=128), [], [Bvt[s]])
            k.dma("sp", zt[s].rearrange("p (a d) -> p a d", d=128),
                  c.hz[tsl, h * 128:(h + 1) * 128].rearrange("(a p) d -> p a d", p=128), [], [Bzt[s]])
            if tb == 0:
                k.memset("dve", GG[s][:, 0:1], 0.0, [BGG[s]])
            else:
                k.cp("dve", GG[s][:, 0:1], GG[1 - s][:, 512:513], [BGG[1 - s]], [BGG[s]])
            k.scan(GG[s][:, 1:513], ones, gt[s], GG[s][:, 0:1], ALU.mult, ALU.add, [Bc, Bgt[s], BGG[s]], [BGG[s]])
            Gv = v3(GG[s][:, 1:513])
            Gb = v3(GG[s][:, 0:512])
            bc = lambda ap: ap.to_broadcast([128, 8, 64])
            k.tt("dve", v3(A), Gv, bc(Gv[:, :, 31:32]), ALU.subtract, [BGG[s]], [BA])
            k.act(E, A, AF.Exp, [BA], [BE])
            k.tt("pool", qtl, qt[s], E, ALU.mult, [Bqt[s], BE], [Bqtl])
            k.act(E, A, AF.Exp, [BA], [BE], scale=-1.0)
            k.tt("pool", ktl, kt[s], E, ALU.mult, [Bkt[s], BE], [Bktl])
            k.tt("dve", v3(A), Gv, bc(Gv[:, :, 63:64]), ALU.subtract, [BGG[s]], [BA])
            k.act(E, A, AF.Exp, [BA], [BE], scale=-1.0)
            k.tt("pool", khT, kt[s], E, ALU.mult, [Bkt[s], BE], [BkhT])
            k.tt("dve", v3(A), Gv, bc(Gb[:, :, 0:1]), ALU.subtract, [BGG[s]], [BA])
            k.act(E4[s], A, AF.Exp, [BA], [BE4[s]])
            k.tt("pool", qh, qt[s], E4[s], ALU.mult, [Bqt[s], BE4[s]], [Bqhh])
            for p in range(4):
                psl = slice(p * 128, (p + 1) * 128)
                vp = vt[s][:, psl]
                b_sc = c.nextbank()
                k.mm(c.ps[b_sc][:, 0:128], ktl[:, psl], qtl[:, psl], True, True, [Bktl, Bqtl], [c.Bps[b_sc]])
                k.tt("dve", scm, c.ps[b_sc][:, 0:128], c.bd[:], ALU.mult, [c.Bps[b_sc], c.B], [Bscm])
                k.tr(c.pst[0][:, 0:128], khT[:, psl], c.ident[:], [BkhT, c.B], [c.Bpst[0]])
                k.cp("act", khA[0:64, :], c.pst[0][0:64, 0:128], [c.Bpst[0]], [Bkh[0]])
                k.cp("act", khB[64:128, :], c.pst[0][64:128, 0:128], [c.Bpst[0]], [Bkh[1]])
                k.cp("pool", qhA[:, 0:64], qh[:, p * 128:p * 128 + 64], [Bqhh], [Bqh[0]])
                k.cp("pool", qhB[:, 64:128], qh[:, p * 128 + 64:p * 128 + 128], [Bqhh], [Bqh[1]])
                b_o = c.nextbank()
                b_kv = c.nextbank()
                k.mm(c.ps[b_o][:, 0:128], scm, vp, True, False, [Bscm, Bvt[s]], [c.Bps[b_o]])
                k.mm(c.ps[b_o][:, 0:128], qhA, Sbf[0], False, False, [Bqh[0], BSb[0]], [c.Bps[b_o]])
                k.mm(c.ps[b_kv][:, 0:128], khA, vp, True, True, [Bkh[0], Bvt[s]], [c.Bps[b_kv]])
                k.stt("dve", S32, S32, E4[s][:, p * 128 + 63:p * 128 + 64], c.ps[b_kv][:, 0:128], ALU.mult, ALU.add,
                      [BS, BE4[s], c.Bps[b_kv]], [BS])
                k.cp("act", Sbf[1], S32, [BS], [BSb[1]])
                k.mm(c.ps[b_o][:, 0:128], qhB, Sbf[1], False, True, [Bqh[1], BSb[1]], [c.Bps[b_o]])
                b_kv2 = c.nextbank()
                k.mm(c.ps[b_kv2][:, 0:128], khB, vp, True, True, [Bkh[1], Bvt[s]], [c.Bps[b_kv2]])
                k.stt("dve", S32, S32, E4[s][:, p * 128 + 127:p * 128 + 128], c.ps[b_kv2][:, 0:128], ALU.mult, ALU.add,
                      [BS, BE4[s], c.Bps[b_kv2]], [BS])
                k.cp("act", Sbf[0], S32, [BS], [BSb[0]])
                t0 = tb * 512 + p * 128
                oi = (it * 4 + p) % 2
                head_epilogue(k, c, c.ps[b_o][:, 0:128], c.Bps[b_o], st, Bst, gainb, Bc, zt[s][:, psl], Bzt[s],
                              o1, Bo1, om[oi], Bom[oi], c.mixed[t0:t0 + 128, col0 + h * 128:col0 + (h + 1) * 128], junk, Bj)


def phase_diff(k, c, lam_d, gain_d, j, layer_idx, L, col0):
    ar = c.arena
    ar.reset()
    k.barrier(c.scr[:, 0:1])
    NHL = c.NHL
    NB = L // 128
    lam_init = 0.8 - 0.6 * math.exp(-0.3 * layer_idx)
    K1 = ar.alloc(L, BF16)
    K2 = ar.alloc(L, BF16)
    QT = ar.alloc(L, BF16)
    VA = ar.alloc(NB * 129, BF16)
    ZT = ar.alloc(NB * 128, BF16)
    lam = ar.alloc(256, F32)
    lt = ar.alloc(128, F32)
    sm = ar.alloc(8, F32)
    gainb = ar.alloc(128, F32)
    st = ar.alloc(8, F32)
    PT = [ar.alloc(512, BF16) for _ in range(2)]
    t2 = ar.alloc(128, F32)
    od = ar.alloc(128, F32)
    o1 = ar.alloc(128, F32)
    junk = ar.alloc(128, BF16)
    om = [ar.alloc(128, BF16) for _ in range(2)]
    B = lambda: k.buf()
    BK1, BK2, BQ, BV, BZ, Bc, Bst, BP, Bt2, Bod, Bo1, Bj, Bom = B(), B(), B(), B(), B(), B(), B(), [B(), B()], B(), B(), B(), B(), [B(), B()]
    VAv = VA.rearrange("p (a d) -> p a d", d=129)
    ZTv = ZT.rearrange("p (a d) -> p a d", d=128)
    k.dma("sp", lam, lam_d[j:j + 1, :, :].rearrange("o a b -> o (a b)").to_broadcast([128, 256]), [], [Bc], slow=True)
    k.tt("dve", lt[:, 0:64], lam[:, 0:64], lam[:, 64:128], ALU.mult, [Bc], [Bc])
    k.tt("dve", lt[:, 64:128], lam[:, 128:192], lam[:, 192:256], ALU.mult, [Bc], [Bc])
    k.red(sm[:, 0:2], lt.rearrange("p (a b) -> p a b", b=64), ALU.add, [Bc], [Bc])
    k.act(sm[:, 2:4], sm[:, 0:2], AF.Exp, [Bc], [Bc])
    k.tt("dve", sm[:, 4:5], sm[:, 3:4], sm[:, 2:3], ALU.subtract, [Bc], [Bc])
    k.ts("dve", sm[:, 5:6], sm[:, 4:5], -lam_init, None, ALU.add, None, [Bc], [Bc])
    k.dma("sp", gainb, gain_d[j:j + 1, :].to_broadcast([128, 128]), [], [Bc], slow=True)
    k.ts("dve", gainb, gainb, 1.0 - lam_init, None, ALU.mult, None, [Bc], [Bc])
    k.memset("pool", K1[64:128, :], 0.0, [BK1])
    k.memset("pool", K2[0:64, :], 0.0, [BK2])
    k.memset("pool", VAv[:, :, 128:129], 1.0, [BV])
    NQ = L // 256
    it = 0
    for h in range(NHL):
        k.dma("sp", K1[0:64, :], c.dkT[h, 0:64, :], [], [BK1])
        k.dma("sp", K2[64:128, :], c.dkT[h, 64:128, :], [], [BK2])
        k.dma("sp", QT, c.dqT[h, :, :], [], [BQ])
        k.dma3("sp", VAv[:, :, 0:128], c.dv[:, h * 128:(h + 1) * 128].rearrange("(a p) d -> p a d", p=128), [], [BV])
        k.dma3("sp", ZTv, c.dz[:, h * 128:(h + 1) * 128].rearrange("(a p) d -> p a d", p=128), [], [BZ])
        for qi in range(NQ):
            q0 = qi * 256
            nkb = 2 * qi + 2
            for kb in range(nkb):
                s = it % 2
                it += 1
                bs = 4 + s
                ksl = slice(kb * 128, (kb + 1) * 128)
                k.mm(c.ps[bs][:, 0:256], K1[:, ksl], QT[:, q0:q0 + 256], True, True, [BK1, BQ], [c.Bps[bs]])
                k.mm(c.ps[bs][:, 256:512], K2[:, ksl], QT[:, q0:q0 + 256], True, True, [BK2, BQ], [c.Bps[bs]])
                k.act(PT[s], c.ps[bs][:, :], AF.Exp, [c.Bps[bs]], [BP[s]])
                dq = kb - 2 * qi
                if dq >= 0:
                    pv = PT[s].rearrange("p (a q) -> p a q", q=256)[:, :, dq * 128:(dq + 1) * 128]
                    k.tt("pool", pv, pv, c.tri[:].unsqueeze(1).to_broadcast([128, 2, 128]), ALU.mult, [BP[s], c.B], [BP[s]])
                for qs in range(2):
                    if dq == 1 and qs == 0:
                        continue
                    last = (kb == 2 * qi + qs)
                    for comp in range(2):
                        bo = qs * 2 + comp
                        k.mm(c.ps[bo][:, 0:129], PT[s][:, comp * 256 + qs * 128:comp * 256 + (qs + 1) * 128], VAv[:, kb, :],
                             kb == 0, last, [BP[s], BV], [c.Bps[bo]])
            for qs in range(2):
                O1, O2 = c.ps[qs * 2], c.ps[qs * 2 + 1]
                B1, B2 = c.Bps[qs * 2], c.Bps[qs * 2 + 1]
                k.recip(sm[:, 6:7], O1[:, 128:129], [B1], [Bc])
                k.recip(sm[:, 7:8], O2[:, 128:129], [B2], [Bc])
                k.tt("dve", sm[:, 7:8], sm[:, 7:8], sm[:, 5:6], ALU.mult, [Bc], [Bc])
                k.act(t2, O2[:, 0:128], AF.Copy, [B2, Bc], [Bt2], scale=sm[:, 7:8])
                k.stt("dve", od, O1[:, 0:128], sm[:, 6:7], t2, ALU.mult, ALU.add, [B1, Bc, Bt2], [Bod])
                t0 = q0 + qs * 128
                oi = (qi * 2 + qs) % 2
                head_epilogue(k, c, od, Bod, st, Bst, gainb, Bc, ZTv[:, qi * 2 + qs, :], BZ, o1, Bo1, om[oi], Bom[oi],
                              c.mixed[t0:t0 + 128, col0 + h * 128:col0 + (h + 1) * 128], junk, Bj)


def phase_outproj(k, c, w_d, gpost_d, li, x_src, x_dst, L):
    ar = c.arena
    ar.reset()
    k.barrier(c.scr[:, 0:1])
    MW = 2 * c.HW
    MC = MW // 128
    wb = ar.alloc(MC * D, BF16)
    wst = [ar.alloc(D, F32) for _ in range(2)]
    gp = ar.alloc(D, F32)
    mt = [ar.alloc(MW, BF16) for _ in range(2)]
    mT = ar.alloc(MC * 128, BF16)
    xt = [ar.alloc(D, F32) for _ in range(2)]
    tmp = ar.alloc(D, F32)
    junk = ar.alloc(512, BF16)
    st = ar.alloc(8, F32)
    B = lambda: k.buf()
    Bw, Bwst, Bg, Bmt, BmT, Bxt, Btmp, Bj, Bst = B(), [B(), B()], B(), [B(), B()], B(), [B(), B()], B(), B(), B()
    wbv = wb.rearrange("p (kc n) -> p kc n", n=D)
    for q in range(MC):
        s = q % 2
        k.dma("sp", wst[s], w_d[q * 128:(q + 1) * 128, :], [], [Bwst[s]])
        k.cp("pool", wbv[:, q, :], wst[s], [Bwst[s]], [Bw])
    k.dma("sp", gp, gpost_d[li:li + 1, :].to_broadcast([128, D]), [], [Bg], slow=True)
    mTv = mT.rearrange("p (kc t) -> p kc t", t=128)
    for t in range(L // 128):
        s = t % 2
        rows = slice(t * 128, (t + 1) * 128)
        k.dma("sp", mt[s], c.mixed[rows, :], [], [Bmt[s]])
        k.dma("sp", xt[s], x_src[rows, :], [], [Bxt[s]])
        for half in range(MC // 8):
            ps = c.pst[half % 2]
            for jj in range(8):
                kc = half * 8 + jj
                k.tr(ps[:, jj * 128:(jj + 1) * 128], mt[s][:, kc * 128:(kc + 1) * 128], c.ident[:], [Bmt[s], c.B], [c.Bpst[half % 2]])
            k.cp("dve", mT[:, half * 1024:(half + 1) * 1024], ps[:, :], [c.Bpst[half % 2]], [BmT])
        for cg in range(4):
            for kc in range(MC):
                k.mm(c.ps[cg][:, :], mTv[:, kc, :], wbv[:, kc, cg * 512:(cg + 1) * 512], kc == 0, kc == MC - 1, [BmT, Bw], [c.Bps[cg]])
            k.act(junk, c.ps[cg][:, :], AF.Square, [c.Bps[cg]], [Bj, Bst], accum=st[:, cg:cg + 1])
        k.red(st[:, 4:5], st[:, 0:4], ALU.add, [Bst], [Bst])
        rstd_from_ssq(k, st[:, 4:5], st[:, 5:6], st[:, 6:7], D, [Bst], [Bst])
        for cg in range(4):
            csl = slice(cg * 512, (cg + 1) * 512)
            k.stt("dve", tmp[:, csl], c.ps[cg][:, :], st[:, 6:7], gp[:, csl], ALU.mult, ALU.mult, [c.Bps[cg], Bst, Bg], [Btmp])
        k.tt("pool", xt[s], xt[s], tmp, ALU.add, [Bxt[s], Btmp], [Bxt[s]])
        k.dma("pool", x_dst[rows, :], xt[s], [Bxt[s]], [])


def even_layer(k, c, x_src, x_dst, gpre_d, gpost_d, w_in_d, w_out_d, lb_d, hgn_d, lam_d, dfn_d, j, layer_idx, L):
    phase_prenorm(k, c, x_src, gpre_d, layer_idx, L)
    slots = [dict(kind="fm", epi=epi_fm_simple(c.qT, None, AF.Silu)),
             dict(kind="fm", epi=epi_hgrn_f(lb_d, j, c.gT, c.kT, None)),
             dict(kind="tm", epi=epi_tm_simple(c.hv, None, AF.Copy)),
             dict(kind="tm", epi=epi_tm_simple(c.hz, None, AF.Silu)),
             dict(kind="fm", epi=epi_fm_simple(c.dqT, None, AF.Copy, scale=0.125)),
             dict(kind="fm", epi=epi_fm_simple(c.dkT, None, AF.Copy)),
             dict(kind="tm", epi=epi_tm_simple(c.dv, None, AF.Copy)),
             dict(kind="tm", epi=epi_tm_simple(c.dz, None, AF.Silu))]
    phase_inproj(k, c, w_in_d, slots, L)
    phase_hgrn(k, c, hgn_d, j, L, col0=0)
    phase_diff(k, c, lam_d, dfn_d, j, layer_idx, L, col0=c.HW)
    phase_outproj(k, c, w_out_d, gpost_d, layer_idx, x_src, x_dst, L)


def setup_consts_odd(k, c):
    nc = k.nc
    B = [c.B]
    c.B.const = False
    c.sut_f = nc.alloc_sbuf_tensor("sut_f", [128, 128], F32)
    c.slt_f = nc.alloc_sbuf_tensor("slt_f", [128, 128], F32)
    c.lte_f = nc.alloc_sbuf_tensor("lte_f", [128, 128], F32)
    c.ones_f = nc.alloc_sbuf_tensor("ones_f", [128, 128], F32)
    c.ones_b = nc.alloc_sbuf_tensor("ones_b", [128, 128], BF16)
    c.nsut = nc.alloc_sbuf_tensor("nsut", [128, 128], F32)
    c.nslt = nc.alloc_sbuf_tensor("nslt", [128, 128], F32)
    c.g1 = nc.alloc_sbuf_tensor("g1", [128, 128], BF16)
    c.g2 = nc.alloc_sbuf_tensor("g2", [128, 128], BF16)
    c.sbm = nc.alloc_sbuf_tensor("sbm", [128, 512], BF16)
    k.memset("pool", c.ones_f[:], 1.0, B)
    k.cp("pool", c.ones_b[:], c.ones_f[:], B, B)
    k.memset("pool", c.sut_f[:], 1.0, B)
    k.asel(c.sut_f[:], c.sut_f[:], [[1, 128]], ALU.is_gt, 0.0, 0, -1, B, B)
    k.memset("pool", c.slt_f[:], 1.0, B)
    k.asel(c.slt_f[:], c.slt_f[:], [[-1, 128]], ALU.is_gt, 0.0, 0, 1, B, B)
    k.ts("pool", c.nsut[:], c.sut_f[:], -1.0, None, ALU.mult, None, B, B)
    k.ts("pool", c.nslt[:], c.slt_f[:], -1.0, None, ALU.mult, None, B, B)
    k.cp("pool", c.g1[:], c.slt_f[:], B, B)
    k.cp("pool", c.g2[:], c.tri_f[:], B, B)
    k.cp("pool", c.sbm[:, 0:128], c.sut_f[:], B, B)
    k.cp("pool", c.sbm[:, 128:256], c.ones_f[:], B, B)
    k.memset("pool", c.sbm[:, 256:384], 0.0, B)
    k.cp("pool", c.sbm[:, 384:512], c.sut_f[:], B, B)
    c.B.const = True


def epi_ab(alog_d, dtb_d, gb_dst):
    def make(k, c, ar):
        NHL = c.NHL
        al = ar.alloc(NHL, F32)
        db = ar.alloc(NHL, F32)
        Bc = k.buf()
        k.dma("sp", al, alog_d.to_broadcast([128, NHL]), [], [Bc], slow=True)
        k.dma("sp", db, dtb_d.to_broadcast([128, NHL]), [], [Bc], slow=True)
        k.act(al, al, AF.Exp, [Bc], [Bc])
        k.ts("dve", al, al, -1.0, None, ALU.mult, None, [Bc], [Bc])
        t = [ar.alloc(2 * NHL, F32) for _ in range(2)]
        Bt = [k.buf(), k.buf()]
        cnt = [0]

        def epi(cg, tt_, ps, Bp):
            s = cnt[0] % 2
            cnt[0] += 1
            k.tt("dve", t[s][:, 0:NHL], ps[:, 0:NHL], db, ALU.add, [Bp, Bc], [Bt[s]])
            k.act(t[s][:, 0:NHL], t[s][:, 0:NHL], AF.Exp, [Bt[s]], [Bt[s]])
            k.act(t[s][:, 0:NHL], t[s][:, 0:NHL], AF.Ln, [Bt[s]], [Bt[s]], bias=1.0)
            k.tt("dve", t[s][:, 0:NHL], t[s][:, 0:NHL], al, ALU.mult, [Bt[s], Bc], [Bt[s]])
            k.act(t[s][:, NHL:2 * NHL], ps[:, NHL:2 * NHL], AF.Sigmoid, [Bp], [Bt[s]])
            k.dma("pool", gb_dst[tt_ * 128:(tt_ + 1) * 128, :], t[s], [Bt[s]], [])
        return epi
    return make


def phase_inproj_ab(k, c, wab_d, alog_d, dtb_d, L):
    ar = c.arena
    ar.reset()
    k.barrier(c.scr[:, 0:1])
    NHL = c.NHL
    NW = 2 * NHL
    wst = ar.alloc(KC * NW, F32)
    wb = ar.alloc(KC * NW, BF16)
    hb = [ar.alloc(KC * 512, BF16) for _ in range(2)]
    Bw, Bhb = k.buf(), [k.buf(), k.buf()]
    k.dma("sp", wst.rearrange("p (kc n) -> p kc n", n=NW), wab_d.rearrange("(kc p) n -> p kc n", p=128), [], [Bw], slow=True)
    k.cp("pool", wb, wst, [Bw], [Bw])
    wbv = wb.rearrange("p (kc n) -> p kc n", n=NW)
    epi = epi_ab(alog_d, dtb_d, c.gb)(k, c, ar)
    for tb in range(L // 512):
        s = tb % 2
        k.dma("sp", hb[s].rearrange("p (kc t) -> p kc t", t=512),
              c.hT[:, :, tb * 512:(tb + 1) * 512].rearrange("kc p t -> p kc t"), [], [Bhb[s]])
        hv = hb[s].rearrange("p (kc t) -> p kc t", t=512)
        for tt_ in range(4):
            pb = c.nextbank()
            for kc in range(KC):
                k.mm(c.ps[pb][:, 0:NW], hv[:, kc, tt_ * 128:(tt_ + 1) * 128], wbv[:, kc, :], kc == 0, kc == KC - 1,
                     [Bw, Bhb[s]], [c.Bps[pb]])
            epi(0, tb * 4 + tt_, c.ps[pb], c.Bps[pb])


def phase_gdn_prep(k, c, conv_d, L):
    ar = c.arena
    ar.reset()
    k.barrier(c.scr[:, 0:1])
    NHL = c.NHL
    cw = ar.alloc(4 * 3 * NHL, F32)
    Bc = k.buf()
    k.dma("sp", cw.rearrange("p (k w h) -> p k w h", k=4, w=3), conv_d.rearrange("k w (h p) -> p k w h", p=128), [], [Bc], slow=True)
    cwv = cw.rearrange("p (k w h) -> p k w h", k=4, w=3)
    raw = [ar.alloc(515, BF16) for _ in range(2)]
    acc = [ar.alloc(512, F32) for _ in range(2)]
    sq = ar.alloc(512, BF16)
    rn = ar.alloc(512, F32)
    ob = [ar.alloc(512, BF16) for _ in range(2)]
    Braw, Bacc, Bsq, Brn, Bob = [k.buf(), k.buf()], [k.buf(), k.buf()], k.buf(), k.buf(), [k.buf(), k.buf()]
    srcs = [c.cqT, c.ckT, c.cvT]
    dsts = [c.qT, c.kT, c.vT]
    it = 0
    for h in range(NHL):
        for w in range(3):
            for tb in range(L // 512):
                s = it % 2
                it += 1
                t0 = tb * 512
                if tb == 0:
                    k.memset("pool", raw[s][:, 0:3], 0.0, [Braw[s]])
                    k.dma("sp", raw[s][:, 3:515], srcs[w][h, :, 0:512], [], [Braw[s]])
                else:
                    k.dma("sp", raw[s][:, 0:515], srcs[w][h, :, t0 - 3:t0 + 512], [], [Braw[s]])
                k.ts("dve", acc[s], raw[s][:, 3:515], cwv[:, 3, w, h:h + 1], None, ALU.mult, None, [Braw[s], Bc], [Bacc[s]])
                for kk in range(3):
                    k.stt("dve", acc[s], raw[s][:, kk:kk + 512], cwv[:, kk, w, h:h + 1], acc[s], ALU.mult, ALU.add,
                          [Braw[s], Bc, Bacc[s]], [Bacc[s]])
                k.act(acc[s], acc[s], AF.Silu, [Bacc[s]], [Bacc[s]])
                if w == 2:
                    k.cp("pool", ob[s], acc[s], [Bacc[s]], [Bob[s]])
                else:
                    k.tt("pool", sq, acc[s], acc[s], ALU.mult, [Bacc[s]], [Bsq])
                    pb = c.nextbank()
                    k.mm(c.ps[pb][:, :], c.ones_b[:], sq, True, True, [Bsq, c.B], [c.Bps[pb]])
                    k.ts("dve", rn, c.ps[pb][:, :], EPS, None, ALU.add, None, [c.Bps[pb]], [Brn])
                    k.recip(rn, rn, [Brn], [Brn])
                    k.act(rn, rn, AF.Sqrt, [Brn], [Brn], scale=(1.0 / 128.0 if w == 0 else 1.0))
                    k.tt("dve", ob[s], acc[s], rn, ALU.mult, [Bacc[s], Brn], [Bob[s]])
                k.dma("pool", dsts[w][h, :, t0:t0 + 512], ob[s], [Bob[s]], [])


def phase_gdn(k, c, gain_d, j, L, col0=0):
    ar = c.arena
    ar.reset()
    k.barrier(c.scr[:, 0:1])
    NHL = c.NHL
    NCH = L // 128
    B = lambda: k.buf()
    gainb = ar.alloc(128, F32)
    Bc = B()
    k.dma("sp", gainb, gain_d[j:j + 1, :].to_broadcast([128, 128]), [], [Bc], slow=True)
    GB = ar.alloc(NCH * 2 * NHL, F32)
    GBv = GB.rearrange("p (a n) -> p a n", n=2 * NHL)
    BGB = B()
    k.dma3("sp", GBv, c.gb.rearrange("(a p) n -> p a n", p=128), [], [BGB])
    sc = ar.alloc(NCH * 5 * NHL, F32)
    scv = sc.rearrange("p (a f h) -> p a f h", f=5, h=NHL)
    Bsc = B()
    for n in range(NCH):
        pb = c.nextbank()
        k.mm(c.ps[pb][:, 0:NHL], c.tri_f[:], GBv[:, n, 0:NHL], True, True, [c.B, BGB], [c.Bps[pb]])
        k.mm(c.ps[pb][:, 64:64 + NHL], c.ones_f[:], GBv[:, n, 0:NHL], True, True, [c.B, BGB], [c.Bps[pb]])
        k.act(scv[:, n, 1, :], c.ps[pb][:, 0:NHL], AF.Exp, [c.Bps[pb]], [Bsc])
        k.ts("dve", scv[:, n, 2, :], scv[:, n, 1, :], -1.0, None, ALU.mult, None, [Bsc], [Bsc])
        k.cp("dve", scv[:, n, 0, :], c.ps[pb][:, 0:NHL], [c.Bps[pb]], [Bsc])
        k.tt("dve", scv[:, n, 3, :], c.ps[pb][:, 64:64 + NHL], scv[:, n, 0, :], ALU.subtract, [c.Bps[pb], Bsc], [Bsc])
        k.act(scv[:, n, 3, :], scv[:, n, 3, :], AF.Exp, [Bsc], [Bsc])
        k.act(scv[:, n, 4, :], c.ps[pb][:, 64:64 + NHL], AF.Exp, [c.Bps[pb]], [Bsc])
    QT = ar.alloc(L, BF16)
    KT = ar.alloc(L, BF16)
    VT = ar.alloc(L, BF16)
    ZT = ar.alloc(NCH * 128, BF16)
    ZTv = ZT.rearrange("p (a d) -> p a d", d=128)
    BQ, BK, BV, BZ = B(), B(), B(), B()
    Gb, nGb, idb = ar.alloc(128, F32), ar.alloc(128, F32), ar.alloc(128, F32)
    Dm, Eup, Elo, tmp1, tmp3, tmp4 = [ar.alloc(128, F32) for _ in range(6)]
    BGb, BD, BEup, BElo, Bt1, Bt3, Bt4 = B(), B(), B(), B(), B(), B(), B()
    M = [ar.alloc(128, F32) for _ in range(7)]
    N = [ar.alloc(128, F32) for _ in range(6)]
    BM = [B() for _ in range(7)]
    BN = [B() for _ in range(6)]
    attnT = ar.alloc(128, F32)
    Battn = B()
    vtm, kdec = ar.alloc(128, BF16), ar.alloc(128, F32)
    Bvtm, Bkdec = B(), B()
    S32 = ar.alloc(128, F32)
    Sbf = ar.alloc(128, BF16)
    BS, BSb = B(), B()
    X = ar.alloc(128, F32)
    Xb = ar.alloc(128, BF16)
    BX, BXb = B(), B()
    inter = ar.alloc(128, F32)
    Bint = B()
    od, o1 = ar.alloc(128, F32), ar.alloc(128, F32)
    junk = ar.alloc(128, BF16)
    om = [ar.alloc(128, BF16) for _ in range(2)]
    st = ar.alloc(8, F32)
    Bod, Bo1, Bj, Bom, Bst = B(), B(), B(), [B(), B()], B()
    for h in range(NHL):
        k.dma("sp", QT, c.qT[h, :, :], [], [BQ])
        k.dma("sp", KT, c.kT[h, :, :], [], [BK])
        k.dma("sp", VT, c.vT[h, :, :], [], [BV])
        k.dma3("sp", ZTv, c.cz[:, h * 128:(h + 1) * 128].rearrange("(a p) d -> p a d", p=128), [], [BZ])
        k.memset("dve", S32, 0.0, [BS])
        k.memset("pool", Sbf, 0.0, [BSb])
        for n in range(NCH):
            csl = slice(n * 128, (n + 1) * 128)
            gcolh = GBv[:, n, h:h + 1]
            betah = GBv[:, n, NHL + h:NHL + h + 1]
            egc, negc, edec, egl = [scv[:, n, f, h:h + 1] for f in (1, 2, 3, 4)]
            k.cp("pool", Gb, gcolh.to_broadcast([128, 128]), [BGB], [BGb])
            k.ts("pool", nGb, Gb, -1.0, None, ALU.mult, None, [BGb], [BGb])
            k.ts("pool", idb, c.ident_f[:], betah, None, ALU.mult, None, [BGB, c.B], [BGb])
            b_dt, b_bb, b_kk, b_at = c.nextbank(), c.nextbank(), c.nextbank(), c.nextbank()
            k.mm(c.ps[b_dt][:, 0:128], Gb, c.tri_f[:], True, False, [BGb, c.B], [c.Bps[b_dt]])
            k.mm(c.ps[b_dt][:, 0:128], c.tri_f[:], nGb, False, True, [BGb, c.B], [c.Bps[b_dt]])
            k.mm(c.ps[b_bb][:, 0:128], c.ones_f[:], idb, True, True, [BGb, c.B], [c.Bps[b_bb]])
            k.mm(c.ps[b_kk][:, 0:128], KT[:, csl], KT[:, csl], True, True, [BK], [c.Bps[b_kk]])
            k.mm(c.ps[b_at][:, 0:128], KT[:, csl], QT[:, csl], True, True, [BK, BQ], [c.Bps[b_at]])
            k.ts("dve", Dm, c.ps[b_dt][:, 0:128], 0.0, None, ALU.min, None, [c.Bps[b_dt]], [BD])
            k.act(Eup, Dm, AF.Exp, [BD], [BEup])
            k.ts("dve", Dm, c.ps[b_dt][:, 0:128], 0.0, None, ALU.max, None, [c.Bps[b_dt]], [BD])
            k.act(Elo, Dm, AF.Exp, [BD], [BElo], scale=-1.0)
            k.tt("pool", tmp1, Eup, c.nsut[:], ALU.mult, [BEup, c.B], [Bt1])
            k.tt("dve", tmp1, tmp1, c.ps[b_bb][:, 0:128], ALU.mult, [Bt1, c.Bps[b_bb]], [Bt1])
            k.tt("dve", M[0], tmp1, c.ps[b_kk][:, 0:128], ALU.mult, [Bt1, c.Bps[b_kk]], [BM[0]])
            k.tt("pool", tmp3, Elo, c.nslt[:], ALU.mult, [BElo, c.B], [Bt3])
            k.stt("dve", N[0], c.ps[b_kk][:, 0:128], betah, tmp3, ALU.mult, ALU.mult, [c.Bps[b_kk], BGB, Bt3], [BN[0]])
            k.tt("pool", tmp4, Eup, c.tri_f[:], ALU.mult, [BEup, c.B], [Bt4])
            k.tt("dve", attnT, tmp4, c.ps[b_at][:, 0:128], ALU.mult, [Bt4, c.Bps[b_at]], [Battn])
            for m in range(6):
                b1 = c.nextbank()
                k.mm(c.ps[b1][:, 0:128], N[m], M[m], True, True, [BN[m], BM[m]], [c.Bps[b1]])
                k.cp("act", M[m + 1], c.ps[b1][:, 0:128], [c.Bps[b1]], [BM[m + 1]])
                if m < 5:
                    b2 = c.nextbank()
                    k.mm(c.ps[b2][:, 0:128], M[m], N[m], True, True, [BN[m], BM[m]], [c.Bps[b2]])
                    k.cp("dve", N[m + 1], c.ps[b2][:, 0:128], [c.Bps[b2]], [BN[m + 1]])
            k.tr(c.pst[0][:, 0:128], VT[:, csl], c.ident[:], [BV, c.B], [c.Bpst[0]])
            k.cp("act", vtm, c.pst[0][:, 0:128], [c.Bpst[0]], [Bvtm])
            k.tr(c.pst[1][:, 0:128], KT[:, csl], c.ident[:], [BK, c.B], [c.Bpst[1]])
            k.act(kdec, c.pst[1][:, 0:128], AF.Copy, [c.Bpst[1], Bsc], [Bkdec], scale=edec)
            b_ks, b_qs = c.nextbank(), c.nextbank()
            k.mm(c.ps[b_ks][:, 0:128], KT[:, csl], Sbf, True, True, [BK, BSb], [c.Bps[b_ks]])
            k.mm(c.ps[b_qs][:, 0:128], QT[:, csl], Sbf, True, True, [BQ, BSb], [c.Bps[b_qs]])
            k.stt("dve", X, c.ps[b_ks][:, 0:128], negc, vtm, ALU.mult, ALU.add, [c.Bps[b_ks], Bsc, Bvtm], [BX])
            k.ts("dve", X, X, betah, None, ALU.mult, None, [BX, BGB], [BX])
            k.act(inter, c.ps[b_qs][:, 0:128], AF.Copy, [c.Bps[b_qs], Bsc], [Bint], scale=egc)
            for m in range(7):
                bx = c.nextbank()
                k.mm(c.ps[bx][:, 0:128], M[m], X, True, True, [BM[m], BX], [c.Bps[bx]])
                k.tt("dve", X, X, c.ps[bx][:, 0:128], ALU.add, [BX, c.Bps[bx]], [BX])
                b_o, b_kv = c.nextbank(), c.nextbank()
            k.mm(c.ps[b_o][:, 0:128], attnT, X, True, True, [Battn, BX], [c.Bps[b_o]])
            k.mm(c.ps[b_kv][:, 0:128], kdec, X, True, True, [Bkdec, BX], [c.Bps[b_kv]])
            k.stt("dve", S32, S32, egl, c.ps[b_kv][:, 0:128], ALU.mult, ALU.add, [BS, Bsc, c.Bps[b_kv]], [BS])
            k.cp("act", Sbf, S32, [BS], [BSb])
            k.tt("dve", od, inter, c.ps[b_o][:, 0:128], ALU.add, [Bint, c.Bps[b_o]], [Bod])
            oi = n % 2
            head_epilogue(k, c, od, Bod, st, Bst, gainb, Bc, ZTv[:, n, :], BZ, o1, Bo1, om[oi], Bom[oi],
                          c.mixed[csl, col0 + h * 128:col0 + (h + 1) * 128], junk, Bj)


def phase_sb(k, c, L, col0):
    ar = c.arena
    ar.reset()
    k.barrier(c.scr[:, 0:1])
    NHL = c.NHL
    NB = L // 128
    B = lambda: k.buf()
    KT = ar.alloc(L, BF16)
    QT = ar.alloc(L, BF16)
    V = ar.alloc(NB * 128, BF16)
    ZT = ar.alloc(NB * 128, BF16)
    Vv = V.rearrange("p (a d) -> p a d", d=128)
    ZTv = ZT.rearrange("p (a d) -> p a d", d=128)
    BK, BQ, BV, BZ = B(), B(), B(), B()
    ez = [ar.alloc(256, F32) for _ in range(2)]
    sp = [ar.alloc(256, F32) for _ in range(2)]
    hi = [ar.alloc(256, BF16) for _ in range(2)]
    lo = [ar.alloc(256, BF16) for _ in range(2)]
    arg = [ar.alloc(256, F32) for _ in range(2)]
    wT = [ar.alloc(256, BF16) for _ in range(2)]
    Bez, Bsp, Bhi, Blo, Barg, Bw = [[B(), B()] for _ in range(6)]
    om = [ar.alloc(128, BF16) for _ in range(2)]
    Bom = [B(), B()]
    NQ = L // 256
    it = 0
    for h in range(NHL):
        k.dma("sp", KT, c.skT[h, :, :], [], [BK])
        k.dma("sp", QT, c.sqT[h, :, :], [], [BQ])
        k.dma3("sp", Vv, c.sv[:, h * 128:(h + 1) * 128].rearrange("(a p) d -> p a d", p=128), [], [BV])
        k.dma3("sp", ZTv, c.sz[:, h * 128:(h + 1) * 128].rearrange("(a p) d -> p a d", p=128), [], [BZ])
        for qi in range(NQ):
            q0 = qi * 256
            kbs = list(range(2 * qi + 1, -1, -1))
            for kb in kbs:
                s = it % 2
                it += 1
                bz = 3 + s
                ksl = slice(kb * 128, (kb + 1) * 128)
                dq = kb - 2 * qi
                k.mm(c.ps[bz][:, 0:256], KT[:, ksl], QT[:, q0:q0 + 256], True, True, [BK, BQ], [c.Bps[bz]])
                k.act(ez[s], c.ps[bz][:, 0:256], AF.Exp, [c.Bps[bz]], [Bez[s]])
                k.act(sp[s], ez[s], AF.Ln, [Bez[s]], [Bsp[s]], bias=1.0)
                k.ts("dve", hi[s], sp[s], -1.0, None, ALU.mult, None, [Bsp[s]], [Bhi[s]])
                k.stt("dve", lo[s], sp[s], -1.0, hi[s], ALU.mult, ALU.subtract, [Bsp[s], Bhi[s]], [Blo[s]])
                if dq >= 0:
                    msk = c.sbm[:, dq * 256:(dq + 1) * 256]
                    k.tt("pool", hi[s], hi[s], msk, ALU.mult, [Bhi[s], c.B], [Bhi[s]])
                    k.tt("pool", lo[s], lo[s], msk, ALU.mult, [Blo[s], c.B], [Blo[s]])
                k.tt("dve", arg[s], c.ps[bz][:, 0:256], sp[s], ALU.subtract, [c.Bps[bz], Bsp[s]], [Barg[s]])
                first = (kb == kbs[0])
                k.mm(c.ps[2][:, 0:256], c.g1[:], hi[s], first, False, [c.B, Bhi[s]], [c.Bps[2]], skip=not first)
                k.mm(c.ps[2][:, 0:256], c.g1[:], lo[s], False, True, [c.B, Blo[s]], [c.Bps[2]], skip=not first)
                k.tt("dve", arg[s], arg[s], c.ps[2][:, 0:256], ALU.add, [Barg[s], c.Bps[2]], [Barg[s]])
                k.act(wT[s], arg[s], AF.Exp, [Barg[s]], [Bw[s]])
                if dq >= 0:
                    k.tt("pool", wT[s], wT[s], c.sbm[:, dq * 256:(dq + 1) * 256], ALU.mult, [Bw[s], c.B], [Bw[s]])
                k.mm(c.ps[2][:, 0:256], c.g2[:], hi[s], False, False, [c.B, Bhi[s]], [c.Bps[2]], skip=True)
                k.mm(c.ps[2][:, 0:256], c.g2[:], lo[s], False, True, [c.B, Blo[s]], [c.Bps[2]], skip=True)
                for qs in range(2):
                    if dq == 1 and qs == 0:
                        continue
                    firstq = (kb == 2 * qi + qs)
                    k.mm(c.ps[qs][:, 0:128], wT[s][:, qs * 128:(qs + 1) * 128], Vv[:, kb, :], firstq, kb == 0,
                         [Bw[s], BV], [c.Bps[qs]])
            for qs in range(2):
                t0 = q0 + qs * 128
                oi = (qi * 2 + qs) % 2
                head_epilogue(k, c, c.ps[qs][:, 0:128], c.Bps[qs], None, None, None, None, ZTv[:, qi * 2 + qs, :], BZ,
                              None, None, om[oi], Bom[oi], c.mixed[t0:t0 + 128, col0 + h * 128:col0 + (h + 1) * 128], None, None, norm=False)


def odd_layer(k, c, x_src, x_dst, gpre_d, gpost_d, w_in_d, wab_d, w_out_d, conv_d, alog_d, dtb_d, gdn_d, j, layer_idx, L):
    phase_prenorm(k, c, x_src, gpre_d, layer_idx, L)
    slots = [dict(kind="fm", epi=epi_fm_simple(c.cqT, None, AF.Copy)),
             dict(kind="fm", epi=epi_fm_simple(c.ckT, None, AF.Copy)),
             dict(kind="fm", epi=epi_fm_simple(c.cvT, None, AF.Copy)),
             dict(kind="tm", epi=epi_tm_simple(c.cz, None, AF.Silu)),
             dict(kind="fm", epi=epi_fm_simple(c.sqT, None, AF.Copy, scale=128 ** -0.5)),
             dict(kind="fm", epi=epi_fm_simple(c.skT, None, AF.Copy)),
             dict(kind="tm", epi=epi_tm_simple(c.sv, None, AF.Copy)),
             dict(kind="tm", epi=epi_tm_simple(c.sz, None, AF.Silu))]
    phase_inproj(k, c, w_in_d, slots, L)
    phase_inproj_ab(k, c, wab_d, alog_d[j:j + 1, :], dtb_d[j:j + 1, :], L)
    phase_gdn_prep(k, c, conv_d[j], L)
    phase_gdn(k, c, gdn_d, j, L, col0=0)
    phase_sb(k, c, L, col0=c.HW)
    phase_outproj(k, c, w_out_d, gpost_d, layer_idx, x_src, x_dst, L)


def make_ctx(k, NHL, L, nbank=6, arena_bytes=150 * 1024):
    nc = k.nc
    c = Ctx()
    c.NHL, c.HW, c.L = NHL, NHL * 128, L
    c.arena = Arena(nc, arena_bytes)
    c.pspair = [nc.alloc_psum_tensor("pspair%d" % i, [128, 1024], F32) for i in range(4)]
    c.ps = [c.pspair[i // 2][:, (i % 2) * 512:(i % 2 + 1) * 512] for i in range(8)]
    c.Bps = [k.buf("ps%d" % i) for i in range(8)]
    c.pst = [c.ps[6 + i].bitcast(BF16) for i in range(2)]
    c.Bpst = [c.Bps[6], c.Bps[7]]
    c._bank = [0]
    c.nbank_rot = nbank

    def nextbank():
        b = c._bank[0] % c.nbank_rot
        c._bank[0] += 1
        return b
    c.nextbank = nextbank
    c.hT = nc.dram_tensor("hT", [KC, 128, L], BF16).ap()
    c.BhT = k.buf("hT")
    setup_consts(k, c)
    return c


def setup_sbm4(k, c):
    nc = k.nc
    B = [c.B]
    c.B.const = False
    c.sbm4 = nc.alloc_sbuf_tensor("sbm4", [128, 4 * 512], BF16)
    for dq in range(4):
        base = dq * 512
        if dq > 0:
            k.memset("pool", c.sbm4[:, base:base + dq * 128], 0.0, B)
        k.cp("pool", c.sbm4[:, base + dq * 128:base + (dq + 1) * 128], c.sut_f[:], B, B)
        if dq < 3:
            k.memset("pool", c.sbm4[:, base + (dq + 1) * 128:base + 512], 1.0, B)
    c.B.const = True


def phase_diff(k, c, lam_d, gain_d, j, layer_idx, L, col0):
    ar = c.arena
    ar.reset()
    k.barrier(c.scr[:, 0:1])
    NHL = c.NHL
    NB = L // 128
    lam_init = 0.8 - 0.6 * math.exp(-0.3 * layer_idx)
    K1 = ar.alloc(L, BF16)
    K2 = ar.alloc(L, BF16)
    QT = ar.alloc(L, BF16)
    VA = ar.alloc(NB * 129, BF16)
    ZT = ar.alloc(NB * 128, BF16)
    lam = ar.alloc(256, F32)
    lt = ar.alloc(128, F32)
    sm = ar.alloc(8, F32)
    gainb = ar.alloc(128, F32)
    st = ar.alloc(8, F32)
    PT = [ar.alloc(1024, BF16) for _ in range(2)]
    t2 = ar.alloc(128, F32)
    od = ar.alloc(128, F32)
    o1 = ar.alloc(128, F32)
    junk = ar.alloc(128, BF16)
    om = [ar.alloc(128, BF16) for _ in range(2)]
    B = lambda: k.buf()
    BK1, BK2, BQ, BV, BZ, Bc, Bst, BP, Bt2, Bod, Bo1, Bj, Bom = B(), B(), B(), B(), B(), B(), B(), [B(), B()], B(), B(), B(), B(), [B(), B()]
    VAv = VA.rearrange("p (a d) -> p a d", d=129)
    ZTv = ZT.rearrange("p (a d) -> p a d", d=128)
    k.dma("sp", lam, lam_d[j:j + 1, :, :].rearrange("o a b -> o (a b)").to_broadcast([128, 256]), [], [Bc], slow=True)
    k.tt("dve", lt[:, 0:64], lam[:, 0:64], lam[:, 64:128], ALU.mult, [Bc], [Bc])
    k.tt("dve", lt[:, 64:128], lam[:, 128:192], lam[:, 192:256], ALU.mult, [Bc], [Bc])
    k.red(sm[:, 0:2], lt.rearrange("p (a b) -> p a b", b=64), ALU.add, [Bc], [Bc])
    k.act(sm[:, 2:4], sm[:, 0:2], AF.Exp, [Bc], [Bc])
    k.tt("dve", sm[:, 4:5], sm[:, 3:4], sm[:, 2:3], ALU.subtract, [Bc], [Bc])
    k.ts("dve", sm[:, 5:6], sm[:, 4:5], -lam_init, None, ALU.add, None, [Bc], [Bc])
    k.dma("sp", gainb, gain_d[j:j + 1, :].to_broadcast([128, 128]), [], [Bc], slow=True)
    k.ts("dve", gainb, gainb, 1.0 - lam_init, None, ALU.mult, None, [Bc], [Bc])
    k.memset("pool", K1[64:128, :], 0.0, [BK1])
    k.memset("pool", K2[0:64, :], 0.0, [BK2])
    k.memset("pool", VAv[:, :, 128:129], 1.0, [BV])
    NQ = L // 512
    acc = lambda a: (4 + a // 3, (a % 3) * 129)
    it = 0
    for h in range(NHL):
        k.dma("sp", K1[0:64, :], c.dkT[h, 0:64, :], [], [BK1])
        k.dma("sp", K2[64:128, :], c.dkT[h, 64:128, :], [], [BK2])
        k.dma("sp", QT, c.dqT[h, :, :], [], [BQ])
        k.dma3("sp", VAv[:, :, 0:128], c.dv[:, h * 128:(h + 1) * 128].rearrange("(a p) d -> p a d", p=128), [], [BV])
        k.dma3("sp", ZTv, c.dz[:, h * 128:(h + 1) * 128].rearrange("(a p) d -> p a d", p=128), [], [BZ])
        for qi in range(NQ):
            q0 = qi * 512
            nkb = 4 * qi + 4
            for bk in (4, 5, 6):
                k.mm(c.ps[bk][:, 0:387], c.zeros[:, 0:128], c.zeros[:, 0:387], True, True, [c.B], [c.Bps[bk]])
            for kb in range(nkb):
                s = it % 2
                it += 1
                b1, b2 = 2 * s, 2 * s + 1
                ksl = slice(kb * 128, (kb + 1) * 128)
                k.mm(c.ps[b1][:, :], K1[:, ksl], QT[:, q0:q0 + 512], True, True, [BK1, BQ], [c.Bps[b1]])
                k.mm(c.ps[b2][:, :], K2[:, ksl], QT[:, q0:q0 + 512], True, True, [BK2, BQ], [c.Bps[b2]])
                k.act(PT[s], c.pspair[s][:, :], AF.Exp, [c.Bps[b1], c.Bps[b2]], [BP[s]])
                dq = kb - 4 * qi
                if dq >= 0:
                    pv = PT[s].rearrange("p (a q) -> p a q", q=512)[:, :, dq * 128:(dq + 1) * 128]
                    k.tt("pool", pv, pv, c.tri[:].unsqueeze(1).to_broadcast([128, 2, 128]), ALU.mult, [BP[s], c.B], [BP[s]])
                for qs in range(max(dq, 0), 4):
                    for comp in range(2):
                        bo, co = acc(qs * 2 + comp)
                        k.mm(c.ps[bo][:, co:co + 129], PT[s][:, comp * 512 + qs * 128:comp * 512 + (qs + 1) * 128], VAv[:, kb, :],
                             False, True, [BP[s], BV], [c.Bps[bo]], skip=True)
            for qs in range(4):
                bo1, co1 = acc(qs * 2)
                bo2, co2 = acc(qs * 2 + 1)
                O1, O2 = c.ps[bo1][:, co1:co1 + 129], c.ps[bo2][:, co2:co2 + 129]
                B1, B2 = c.Bps[bo1], c.Bps[bo2]
                k.recip(sm[:, 6:7], O1[:, 128:129], [B1], [Bc])
                k.recip(sm[:, 7:8], O2[:, 128:129], [B2], [Bc])
                k.tt("dve", sm[:, 7:8], sm[:, 7:8], sm[:, 5:6], ALU.mult, [Bc], [Bc])
                k.act(t2, O2[:, 0:128], AF.Copy, [B2, Bc], [Bt2], scale=sm[:, 7:8])
                k.stt("dve", od, O1[:, 0:128], sm[:, 6:7], t2, ALU.mult, ALU.add, [B1, Bc, Bt2], [Bod])
                t0 = q0 + qs * 128
                oi = qs % 2
                head_epilogue(k, c, od, Bod, st, Bst, gainb, Bc, ZTv[:, qi * 4 + qs, :], BZ, o1, Bo1, om[oi], Bom[oi],
                              c.mixed[t0:t0 + 128, col0 + h * 128:col0 + (h + 1) * 128], junk, Bj)


def phase_sb(k, c, L, col0):
    ar = c.arena
    ar.reset()
    k.barrier(c.scr[:, 0:1])
    NHL = c.NHL
    NB = L // 128
    B = lambda: k.buf()
    KT = ar.alloc(L, BF16)
    QT = ar.alloc(L, BF16)
    V = ar.alloc(NB * 128, BF16)
    ZT = ar.alloc(NB * 128, BF16)
    Vv = V.rearrange("p (a d) -> p a d", d=128)
    ZTv = ZT.rearrange("p (a d) -> p a d", d=128)
    BK, BQ, BV, BZ = B(), B(), B(), B()
    W = 512
    ez = [ar.alloc(W, F32) for _ in range(2)]
    sp = [ar.alloc(W, F32) for _ in range(2)]
    hi = [ar.alloc(W, BF16) for _ in range(2)]
    lo = [ar.alloc(W, BF16) for _ in range(2)]
    arg = [ar.alloc(W, F32) for _ in range(2)]
    wT = [ar.alloc(W, BF16) for _ in range(2)]
    Bez, Bsp, Bhi, Blo, Barg, Bw = [[B(), B()] for _ in range(6)]
    om = [ar.alloc(128, BF16) for _ in range(2)]
    Bom = [B(), B()]
    NQ = L // W
    it = 0
    for h in range(NHL):
        k.dma("sp", KT, c.skT[h, :, :], [], [BK])
        k.dma("sp", QT, c.sqT[h, :, :], [], [BQ])
        k.dma3("sp", Vv, c.sv[:, h * 128:(h + 1) * 128].rearrange("(a p) d -> p a d", p=128), [], [BV])
        k.dma3("sp", ZTv, c.sz[:, h * 128:(h + 1) * 128].rearrange("(a p) d -> p a d", p=128), [], [BZ])
        for qi in range(NQ):
            q0 = qi * W
            kbs = list(range(4 * qi + 3, -1, -1))
            k.mm(c.ps[3][:, :], c.zeros[:, 0:128], c.zeros[:, 0:512], True, True, [c.B], [c.Bps[3]])
            for kb in kbs:
                s = it % 2
                it += 1
                bz = s
                ksl = slice(kb * 128, (kb + 1) * 128)
                dq = kb - 4 * qi
                k.mm(c.ps[bz][:, :], KT[:, ksl], QT[:, q0:q0 + W], True, True, [BK, BQ], [c.Bps[bz]])
                k.act(ez[s], c.ps[bz][:, :], AF.Exp, [c.Bps[bz]], [Bez[s]])
                k.act(sp[s], ez[s], AF.Ln, [Bez[s]], [Bsp[s]], bias=1.0)
                k.ts("dve", hi[s], sp[s], -1.0, None, ALU.mult, None, [Bsp[s]], [Bhi[s]])
                k.stt("dve", lo[s], sp[s], -1.0, hi[s], ALU.mult, ALU.subtract, [Bsp[s], Bhi[s]], [Blo[s]])
                if dq >= 0:
                    msk = c.sbm4[:, dq * 512:(dq + 1) * 512]
                    k.tt("pool", hi[s], hi[s], msk, ALU.mult, [Bhi[s], c.B], [Bhi[s]])
                    k.tt("pool", lo[s], lo[s], msk, ALU.mult, [Blo[s], c.B], [Blo[s]])
                k.tt("dve", arg[s], c.ps[bz][:, :], sp[s], ALU.subtract, [c.Bps[bz], Bsp[s]], [Barg[s]])
                first = (kb == kbs[0])
                k.mm(c.ps[2][:, :], c.g1[:], hi[s], first, False, [c.B, Bhi[s]], [c.Bps[2]], skip=not first)
                k.mm(c.ps[2][:, :], c.g1[:], lo[s], False, True, [c.B, Blo[s]], [c.Bps[2]], skip=not first)
                k.tt("dve", arg[s], arg[s], c.ps[2][:, :], ALU.add, [Barg[s], c.Bps[2]], [Barg[s]])
                k.act(wT[s], arg[s], AF.Exp, [Barg[s]], [Bw[s]])
                if dq >= 0:
                    k.tt("pool", wT[s], wT[s], c.sbm4[:, dq * 512:(dq + 1) * 512], ALU.mult, [Bw[s], c.B], [Bw[s]])
                if kb > 0:
                    k.mm(c.ps[2][:, :], c.g2[:], hi[s], False, False, [c.B, Bhi[s]], [c.Bps[2]], skip=True)
                    k.mm(c.ps[2][:, :], c.g2[:], lo[s], False, True, [c.B, Blo[s]], [c.Bps[2]], skip=True)
                for qs in range(max(dq, 0), 4):
                    k.mm(c.ps[3][:, qs * 128:(qs + 1) * 128], wT[s][:, qs * 128:(qs + 1) * 128], Vv[:, kb, :], False, True,
                         [Bw[s], BV], [c.Bps[3]], skip=True)
            for qs in range(4):
                t0 = q0 + qs * 128
                oi = qs % 2
                head_epilogue(k, c, c.ps[3][:, qs * 128:(qs + 1) * 128], c.Bps[3], None, None, None, None, ZTv[:, qi * 4 + qs, :], BZ,
                              None, None, om[oi], Bom[oi], c.mixed[t0:t0 + 128, col0 + h * 128:col0 + (h + 1) * 128], None, None, norm=False)


L_FULL = 8192
NHL_FULL = 8
DEPTH = 4
_NC = None


def build_program():
    nc = bass.Bass("TRN2", target_bir_lowering=False)
    k = K(nc)
    L, NHL = L_FULL, NHL_FULL
    HW = NHL * 128
    ext = lambda n, s: nc.dram_tensor(n, s, F32, kind="ExternalInput").ap()
    x = ext("x", [L, D])
    gpre = ext("norm_pre", [DEPTH, D])
    gpost = ext("norm_post", [DEPTH, D])
    ev_w_in = ext("ev_w_in", [2, D, 8 * HW])
    ev_w_out = ext("ev_w_out", [2, 2 * HW, D])
    lb = ext("hg_lb_logits", [2, HW])
    hgn = ext("hg_norm", [2, 128])
    lam = ext("df_lambda", [2, 4, 64])
    dfn = ext("df_norm", [2, 128])
    od_w_in = ext("od_w_main", [2, D, 8 * HW])
    od_wab = ext("od_w_ab", [2, D, 2 * NHL])
    od_w_out = ext("od_w_out", [2, 2 * HW, D])
    conv = ext("gd_conv", [2, 4, 3, HW])
    alog = ext("gd_a_log", [2, NHL])
    dtb = ext("gd_dt_bias", [2, NHL])
    gdn = ext("gd_norm", [2, 128])
    out = nc.dram_tensor("out", [L, D], F32, kind="ExternalOutput").ap()
    xres = nc.dram_tensor("xres", [L, D], F32).ap()
    c = make_ctx(k, NHL, L)
    alloc_scratch(k, c, L)
    setup_consts_odd(k, c)
    setup_sbm4(k, c)
    for layer in range(DEPTH):
        j = layer // 2
        src = x if layer == 0 else xres
        dst = out if layer == DEPTH - 1 else xres
        if layer % 2 == 0:
            even_layer(k, c, src, dst, gpre, gpost, ev_w_in[j], ev_w_out[j], lb, hgn, lam, dfn, j, layer, L)
        else:
            odd_layer(k, c, src, dst, gpre, gpost, od_w_in[j], od_wab[j], od_w_out[j], conv, alog, dtb, gdn, j, layer, L)
    last = k.barrier(c.scr[:, 0:1])
    k.S.emit(final_waits=[last])
    return nc


def kernel(x, norm_pre, norm_post, ev_w_in, ev_w_out, hg_lb_logits, hg_norm, df_lambda, df_norm,
           od_w_in, od_w_out, gd_conv, gd_a_log, gd_dt_bias, gd_norm):
    global _NC
    if _NC is None:
        _NC = build_program()
    f = lambda a: np.ascontiguousarray(np.asarray(a, dtype=np.float32))
    od_w_in = np.asarray(od_w_in, dtype=np.float32)
    od_main = np.ascontiguousarray(np.concatenate([od_w_in[:, :, 0:4096], od_w_in[:, :, 4112:8208]], axis=2))
    od_ab = np.ascontiguousarray(od_w_in[:, :, 4096:4112])
    shared = {"norm_pre": f(norm_pre), "norm_post": f(norm_post), "ev_w_in": f(ev_w_in), "ev_w_out": f(ev_w_out),
              "hg_lb_logits": f(hg_lb_logits), "hg_norm": f(hg_norm), "df_lambda": f(df_lambda), "df_norm": f(df_norm),
              "od_w_main": od_main, "od_w_ab": od_ab, "od_w_out": f(od_w_out),
              "gd_conv": f(np.asarray(gd_conv, dtype=np.float32).reshape(2, 4, 3, 1024)),
              "gd_a_log": f(gd_a_log), "gd_dt_bias": f(gd_dt_bias), "gd_norm": f(gd_norm)}
    xs = np.asarray(x, dtype=np.float32)
    nb = xs.shape[0]
    in_maps = [dict(shared, x=np.ascontiguousarray(xs[b])) for b in range(nb)]
    res = run_bass_kernel_spmd(_NC, in_maps, core_ids=list(range(nb)))
    return np.stack([np.asarray(res.results[b]["out"], dtype=np.float32) for b in range(nb)], axis=0)
```
